# Optimizing a Trainium2 kernel written in Bass

```python
import jax, jax.numpy as jnp
from jax import lax
import numpy as np

D_MODEL = 2048
BATCH = 16
SEQ = 2048
DEPTH = 2

PLE_DIM = 256
FOX_HEADS = 16
FOX_HEAD_DIM = 64
FOX_W = FOX_HEADS * FOX_HEAD_DIM
Q_BLOCK = 128
RWKV_HEADS = 16
RWKV_HEAD_SIZE = 64
RWKV_W = RWKV_HEADS * RWKV_HEAD_SIZE
RWKV_DECAY_LORA = 64
RWKV_ICLR_LORA = 64
RWKV_SHIFT_COLS = 3 * RWKV_W + RWKV_DECAY_LORA + RWKV_ICLR_LORA
AB_COLS = 4 * FOX_W + FOX_HEADS + RWKV_SHIFT_COLS + RWKV_W
AB_OUT = FOX_W + RWKV_W
GMLP_W = D_MODEL
GMLP_GROUPS = 16
GMLP_GROUP_CH = GMLP_W // GMLP_GROUPS
GMLP_CHUNK = 128
C_COLS = 3 * GMLP_W
N_AB_LAYERS = (DEPTH + 1) // 2
N_C_LAYERS = DEPTH // 2
RMS_EPS = 1e-6
LN_EPS = 1e-5
RWKV_GN_EPS = 64e-5

kernel_name = 'hybrid_fox_rwkv7_gmlp_sandwich_ple'


def _split(x, sizes):
    offs = []
    o = 0
    for s in sizes[:-1]:
        o += s
        offs.append(o)
    return jnp.split(x, offs, axis=-1)


def rms_norm(x, g):
    xf = x.astype(jnp.float32)
    y = xf * lax.rsqrt(jnp.mean(xf * xf, axis=-1, keepdims=True) + RMS_EPS)
    return (y * g.astype(jnp.float32)).astype(x.dtype)


def layer_norm(x, g, b):
    xf = x.astype(jnp.float32)
    mu = jnp.mean(xf, axis=-1, keepdims=True)
    var = jnp.mean(jnp.square(xf - mu), axis=-1, keepdims=True)
    y = (xf - mu) * lax.rsqrt(var + LN_EPS) * g.astype(jnp.float32) + b.astype(jnp.float32)
    return y.astype(x.dtype)


def forgetting_attention(q, k, v, log_f):
    T = q.shape[1]
    c = jnp.cumsum(log_f, axis=1).transpose(0, 2, 1)
    scale = FOX_HEAD_DIM ** -0.5
    outs = []
    for blk in range(T // Q_BLOCK):
        q0 = blk * Q_BLOCK
        q1 = q0 + Q_BLOCK
        s = jnp.einsum('bqhd,bkhd->bhqk', q[:, q0:q1], k[:, :q1]).astype(jnp.float32) * scale
        bias = c[:, :, q0:q1, None] - c[:, :, None, :q1]
        causal = (q0 + jnp.arange(Q_BLOCK))[:, None] >= jnp.arange(q1)[None, :]
        s = jnp.where(causal, s + bias, -jnp.inf)
        pr = jax.nn.softmax(s, axis=-1)
        outs.append(jnp.einsum('bhqk,bkhd->bqhd', pr.astype(v.dtype), v[:, :q1]))
    return jnp.concatenate(outs, axis=1)


def rwkv7_time_mix(sh, mu, w0, w2, a0, a2, k_k, k_a, r_k, ln_g, ln_b):
    dt = sh.dtype
    B, T, _ = sh.shape
    H, N = RWKV_HEADS, RWKV_HEAD_SIZE
    prev = jnp.pad(sh, ((0, 0), (1, 0), (0, 0)))[:, :-1]
    sh = (sh + (prev - sh) * mu).astype(jnp.float32)
    r, k, v, w_lo, a_lo = _split(sh, (RWKV_W, RWKV_W, RWKV_W, RWKV_DECAY_LORA, RWKV_ICLR_LORA))
    w_raw = -jax.nn.softplus(-(w0.astype(jnp.float32) + jnp.tanh(w_lo) @ w2.astype(jnp.float32))) - 0.5
    decay = jnp.exp(-jnp.exp(w_raw))
    a = jax.nn.sigmoid(a0.astype(jnp.float32) + a_lo @ a2.astype(jnp.float32))
    kk = (k * k_k.astype(jnp.float32)).reshape(B, T, H, N)
    kk = kk / jnp.maximum(jnp.sqrt(jnp.sum(kk * kk, axis=-1, keepdims=True)), 1e-12)
    k = k * (1.0 + (a - 1.0) * k_a.astype(jnp.float32))
    heads = lambda z: z.reshape(B, T, H, N)
    r_h, k_h, v_h, a_h, w_h = heads(r), heads(k), heads(v), heads(a), heads(decay)
    xs = tuple(z.transpose(1, 0, 2, 3) for z in (r_h, w_h, k_h, v_h, kk, a_h))

    def step(S, inp):
        r_t, w_t, k_t, v_t, kk_t, a_t = inp
        sa = jnp.einsum('bhij,bhj->bhi', S, kk_t)
        S = S * w_t[:, :, None, :] - sa[..., None] * (kk_t * a_t)[:, :, None, :] + v_t[..., None] * k_t[:, :, None, :]
        y = jnp.einsum('bhij,bhj->bhi', S, r_t)
        return S, y

    S0 = jnp.zeros((B, H, N, N), jnp.float32)
    _, y = lax.scan(step, S0, xs)
    y = y.transpose(1, 0, 2, 3)
    mean = jnp.mean(y, axis=-1, keepdims=True)
    var = jnp.mean(jnp.square(y - mean), axis=-1, keepdims=True)
    y = ((y - mean) * lax.rsqrt(var + RWKV_GN_EPS)).reshape(B, T, RWKV_W)
    y = y * ln_g.astype(jnp.float32) + ln_b.astype(jnp.float32)
    bonus = jnp.sum(r_h * k_h * r_k.astype(jnp.float32), axis=-1, keepdims=True) * v_h
    y = y + bonus.reshape(B, T, RWKV_W)
    return y.astype(dt)


def chunked_spatial_gating(u, v, ln_g, ln_b, w_s, b_s):
    B, T, _ = v.shape
    u = jax.nn.gelu(u, approximate=False)
    v = layer_norm(jax.nn.gelu(v, approximate=False), ln_g, ln_b)
    vc = v.reshape(B, T // GMLP_CHUNK, GMLP_CHUNK, GMLP_GROUPS, GMLP_GROUP_CH)
    causal = jnp.tril(jnp.ones((GMLP_CHUNK, GMLP_CHUNK), w_s.dtype))
    mixed = jnp.einsum('gts,bnsgc->bntgc', w_s * causal, vc) + b_s.T[:, :, None]
    return u * mixed.reshape(B, T, GMLP_W)


def setup_inputs(seed: int = 0) -> dict:
    key = jax.random.key(seed)
    ks = iter(jax.random.split(key, 40))
    nrm = lambda shape, scale: scale * jax.random.normal(next(ks), shape, jnp.float32)
    uni = lambda shape, lo, hi: jax.random.uniform(next(ks), shape, jnp.float32, lo, hi)
    E, C = N_AB_LAYERS, N_C_LAYERS
    return {
        'x': nrm((BATCH, SEQ, D_MODEL), 1.0),
        'p': nrm((DEPTH, BATCH, SEQ, PLE_DIM), 1.0),
        'norm_pre': 1.0 + nrm((DEPTH, D_MODEL), 0.02),
        'norm_post': 1.0 + nrm((DEPTH, D_MODEL), 0.02),
        'ab_w_in': nrm((E, D_MODEL, AB_COLS), D_MODEL ** -0.5),
        'fox_f_bias': uni((E, FOX_HEADS), 1.0, 5.0),
        'rwkv_mu': uni((E, RWKV_SHIFT_COLS), 0.0, 1.0),
        'rwkv_w0': uni((E, RWKV_W), -6.0, 1.0),
        'rwkv_w2': nrm((E, RWKV_DECAY_LORA, RWKV_W), 0.1),
        'rwkv_a0': nrm((E, RWKV_W), 0.1),
        'rwkv_a2': nrm((E, RWKV_ICLR_LORA, RWKV_W), 0.1),
        'rwkv_k_k': 0.85 + nrm((E, RWKV_W), 0.05),
        'rwkv_k_a': 1.0 + nrm((E, RWKV_W), 0.05),
        'rwkv_r_k': nrm((E, RWKV_HEADS, RWKV_HEAD_SIZE), 0.1),
        'rwkv_ln_g': 1.0 + nrm((E, RWKV_W), 0.02),
        'rwkv_ln_b': nrm((E, RWKV_W), 0.02),
        'ab_w_out': nrm((E, AB_OUT, D_MODEL), AB_OUT ** -0.5),
        'c_w_in': nrm((C, D_MODEL, C_COLS), D_MODEL ** -0.5),
        'c_ln_g': 1.0 + nrm((C, GMLP_W), 0.02),
        'c_ln_b': nrm((C, GMLP_W), 0.02),
        'c_w_s': nrm((C, GMLP_GROUPS, GMLP_CHUNK, GMLP_CHUNK), GMLP_CHUNK ** -0.5),
        'c_b_s': 1.0 + nrm((C, GMLP_GROUPS, GMLP_CHUNK), 0.01),
        'c_w_out': nrm((C, GMLP_W, D_MODEL), GMLP_W ** -0.5),
        'ple_w_proj': nrm((DEPTH, PLE_DIM, D_MODEL), PLE_DIM ** -0.5),
        'ple_w_gate': nrm((DEPTH, D_MODEL, D_MODEL), D_MODEL ** -0.5),
    }


def reference(x, p, norm_pre, norm_post, ab_w_in, fox_f_bias, rwkv_mu, rwkv_w0, rwkv_w2, rwkv_a0, rwkv_a2,
              rwkv_k_k, rwkv_k_a, rwkv_r_k, rwkv_ln_g, rwkv_ln_b, ab_w_out, c_w_in, c_ln_g, c_ln_b, c_w_s,
              c_b_s, c_w_out, ple_w_proj, ple_w_gate):
    B, T, _ = x.shape
    h = x
    for i in range(DEPTH):
        j = i // 2
        xn = rms_norm(h, norm_pre[i])
        if i % 2 == 0:
            proj = xn @ ab_w_in[j]
            qa, ka, va, fa, ga, shb, gb = _split(
                proj, (FOX_W, FOX_W, FOX_W, FOX_HEADS, FOX_W, RWKV_SHIFT_COLS, RWKV_W))
            log_f = jax.nn.log_sigmoid((fa + fox_f_bias[j]).astype(jnp.float32))
            hd = lambda z: z.reshape(B, T, FOX_HEADS, FOX_HEAD_DIM)
            oa = forgetting_attention(hd(qa), hd(ka), hd(va), log_f).reshape(B, T, FOX_W)
            ob = rwkv7_time_mix(shb, rwkv_mu[j], rwkv_w0[j], rwkv_w2[j], rwkv_a0[j], rwkv_a2[j],
                                rwkv_k_k[j], rwkv_k_a[j], rwkv_r_k[j], rwkv_ln_g[j], rwkv_ln_b[j])
            y = jnp.concatenate([oa * jax.nn.silu(ga), ob * jax.nn.silu(gb)], axis=-1) @ ab_w_out[j]
        else:
            proj = xn @ c_w_in[j]
            u, v, g = _split(proj, (GMLP_W, GMLP_W, GMLP_W))
            oc = chunked_spatial_gating(u, v, c_ln_g[j], c_ln_b[j], c_w_s[j], c_b_s[j])
            y = (oc * jax.nn.silu(g)) @ c_w_out[j]
        h = h + rms_norm(y, norm_post[i])
        h = h + (p[i] @ ple_w_proj[i]) * jax.nn.sigmoid(h @ ple_w_gate[i])
    return h
```

```python
import contextlib
import numpy as np
import concourse.bass as bass
import concourse.mybir as mybir
from concourse.bass_utils import run_bass_kernel_spmd

F32 = mybir.dt.float32
BF16 = mybir.dt.bfloat16
AF = mybir.ActivationFunctionType
ALU = mybir.AluOpType

ENGS = ["pe", "act", "dve", "pool", "sp"]


def A(*args, **kw):
    return args, kw


class _Op:
    __slots__ = ("eng", "fn", "deps", "sig", "pos", "is_dma", "dsem", "dcount", "waits", "count")

    def __init__(self, eng, fn, is_dma=False):
        self.eng = eng
        self.fn = fn
        self.deps = []
        self.sig = False
        self.pos = 0
        self.is_dma = is_dma
        self.dsem = None
        self.dcount = 0
        self.waits = []
        self.count = 0


class Prog:
    def __init__(self, nc, ndma=None):
        self.nc = nc
        self.stacks = [contextlib.ExitStack()]
        self.streams = {e: [] for e in ENGS}
        self.lastw = {}
        self.readers = {}
        self.ndma = ndma or {"sp": 16, "act": 8, "pool": 8}
        self.dma_hist = {e: [] for e in ENGS}
        self.all_dmas = []
        self.ntile = 0

    @contextlib.contextmanager
    def scope(self):
        st = contextlib.ExitStack()
        self.stacks.append(st)
        try:
            yield
        finally:
            self.stacks.pop()
            st.close()

    def sbuf(self, shape, dtype, name=None):
        self.ntile += 1
        return self.stacks[-1].enter_context(self.nc.sbuf_tensor(f"{name or 't'}_{self.ntile}", list(shape), dtype))

    def psum(self, shape, dtype, name=None):
        self.ntile += 1
        return self.stacks[-1].enter_context(self.nc.psum_tensor(f"{name or 'p'}_{self.ntile}", list(shape), dtype))

    EXCL = ("psr", "pproj", "pst", "po", "pv", "pT", "ptrt")

    def _add(self, op, r, w):
        ex = [k for k in r if isinstance(k, tuple) and k[0] in self.EXCL]
        if ex:
            r = [k for k in r if k not in ex]
            w = list(w) + [k for k in ex if k not in w]
        st = self.streams[op.eng]
        op.pos = len(st)
        st.append(op)
        deps = []
        for k in r:
            lw = self.lastw.get(k)
            if lw is not None:
                deps.append((lw, "raw"))
        for k in w:
            lw = self.lastw.get(k)
            if lw is not None:
                deps.append((lw, "waw"))
            for rd in self.readers.get(k, ()):
                deps.append((rd, "war"))
        for d, kind in deps:
            if d is op:
                continue
            if (not d.is_dma) and d.eng == op.eng:
                if op.eng == "pe":
                    continue
            op.deps.append(d)
        for k in w:
            self.lastw[k] = op
            self.readers[k] = []
        for k in r:
            self.readers.setdefault(k, []).append(op)
        return op

    def op(self, eng, fn, r=(), w=()):
        return self._add(_Op(eng, fn), list(r), list(w))

    def pe(self, fn, r=(), w=()):
        return self.op("pe", fn, r, w)

    def act(self, fn, r=(), w=()):
        return self.op("act", fn, r, w)

    def dve(self, fn, r=(), w=()):
        return self.op("dve", fn, r, w)

    def pool(self, fn, r=(), w=()):
        return self.op("pool", fn, r, w)

    def _imm(self, eng, meth, a, r, w):
        args, kw = a
        return self.op(eng, (lambda e, meth=meth, args=args, kw=kw: getattr(e, meth)(*args, **kw)), r, w)

    def pei(self, meth, a, r=(), w=()):
        return self._imm("pe", meth, a, r, w)

    def acti(self, meth, a, r=(), w=()):
        return self._imm("act", meth, a, r, w)

    def dvei(self, meth, a, r=(), w=()):
        return self._imm("dve", meth, a, r, w)

    def pooli(self, meth, a, r=(), w=()):
        return self._imm("pool", meth, a, r, w)

    def dma(self, q, out, in_, r=(), w=(), **kw):
        op = _Op(q, (lambda e, out=out, in_=in_, kw=kw: e.dma_start(out=out, in_=in_, **kw)), is_dma=True)
        hist = self.dma_hist[q]
        n = self.ndma[q]
        op.dsem = (q, len(hist) % n)
        op.dcount = 16 * (len(hist) // n + 1)
        self._add(op, list(r), list(w))
        if len(hist) >= n:
            op.deps.append(hist[len(hist) - n])
        hist.append(op)
        self.all_dmas.append(op)
        return op

    def barrier(self):
        lasts = []
        for e in ["pe", "act", "dve", "pool"]:
            for op in reversed(self.streams[e]):
                if not op.is_dma:
                    lasts.append(op)
                    break
        tails = {}
        for d in self.all_dmas:
            tails[d.dsem] = d
        for e in ENGS:
            op = _Op(e, lambda eng: eng.nop())
            st = self.streams[e]
            op.pos = len(st)
            st.append(op)
            op.deps = [d for d in lasts if d.eng != e] + list(tails.values())
        self.lastw = {}
        self.readers = {}

    def emit(self):
        nc = self.nc
        for e in ENGS:
            seen = {}
            for op in self.streams[e]:
                need = {}
                for d in op.deps:
                    src = d.dsem if d.is_dma else d.eng
                    val = d.dcount if d.is_dma else d.pos
                    if src not in need or val > need[src][0]:
                        need[src] = (val, d)
                op.waits = []
                for src, (val, d) in need.items():
                    if seen.get(src, -1) >= val:
                        continue
                    seen[src] = val
                    if not d.is_dma:
                        d.sig = True
                    op.waits.append(d)
        tail = {}
        for d in self.all_dmas:
            tail[d.dsem] = d
        for e in ENGS:
            c = 0
            for op in self.streams[e]:
                if op.sig:
                    c += 1
                    op.count = c
        stack = self.stacks[0]
        sems = {}
        for e in ["pe", "act", "dve", "pool"]:
            sems[e] = stack.enter_context(nc.semaphore(f"s_{e}"))
        for q, n in self.ndma.items():
            for i in range(n):
                sems[(q, i)] = stack.enter_context(nc.semaphore(f"d_{q}{i}"))
        streams = self.streams

        def run(eh, e):
            for op in streams[e]:
                for d in op.waits:
                    if d.is_dma:
                        eh.wait_ge(sems[d.dsem], d.dcount)
                    else:
                        eh.wait_ge(sems[d.eng], d.count)
                ins = op.fn(eh)
                if op.is_dma:
                    ins.then_inc(sems[op.dsem], 16)
                elif op.sig:
                    ins.then_inc(sems[op.eng], 1)
            if e == "sp":
                for key, d in tail.items():
                    eh.wait_ge(sems[key], d.dcount)

        with nc.Block() as block:
            @block.tensor
            def _(eng):
                run(eng, "pe")

            @block.scalar
            def _(eng):
                run(eng, "act")

            @block.vector
            def _(eng):
                run(eng, "dve")

            @block.gpsimd
            def _(eng):
                run(eng, "pool")

            @block.sync
            def _(eng):
                run(eng, "sp")

    def close(self):
        self.stacks[0].close()


T = 2048
D = 2048
KC = 16
NT = 16
PLE = 256
C0 = float(np.exp(-0.5))
RMS_EPS = 1e-6
LN_EPS = 1e-5
GN_EPS = 64e-5

BQ, BK, BG, BR, BKB, BVB, BLO, BGB = 0, 8, 16, 24, 32, 40, 48, 49
NSTAT = 57
COL_Q, COL_K, COL_V, COL_F, COL_G = 0, 1024, 2048, 3072, 3088
COL_SH, COL_GB = 4112, 7312

OFF_STAT = 0
OFF_WV = OFF_STAT + NSTAT * 2048
OFF_WO0 = OFF_WV + 2 * 8192
OFF_WG0 = OFF_WO0 + 4 * 8192
OFF_CIN = OFF_WG0 + 4 * 8192
OFF_CO = OFF_CIN + 12 * 8192
OFF_WG1 = OFF_CO + 4 * 8192
OFF_WP0 = OFF_WG1 + 4 * 8192
OFF_WP1 = OFF_WP0 + 4 * 1024
OFF_LORA = OFF_WP1 + 4 * 1024
OFF_WF = OFF_LORA + 1024
OFF_END = OFF_WF + 256
WT = ((OFF_END + 2047) // 2048) * 2048

PP_MU, PP_W0, PP_A0, PP_KK, PP_KA, PP_RK, PP_LNG, PP_LNB, PP_BS = 0, 25, 33, 41, 49, 57, 65, 73, 81
NPP = 97
CS_ID, CS_SU, CS_UI, CS_SL, CS_B64, CS_CM = 0, 128, 256, 384, 512, 640
NCS = 640 + 1024


def _stat_blocks(W, c0, ncols):
    Wc = W[:, c0:c0 + ncols]
    nb = ncols // 128
    return Wc.reshape(KC, 128, nb, 128).transpose(1, 2, 0, 3).reshape(128, nb * KC * 128)


def _mov_blocks(W):
    K, N = W.shape
    kc = K // 128
    nct = N // 512
    return W.reshape(kc, 128, nct, 512).transpose(1, 2, 0, 3).reshape(128, nct * kc * 512)


def host_prep(inp):
    W = inp["ab_w_in"][0]
    parts = []
    for c0, n in [(COL_Q, 1024), (COL_K, 1024), (COL_G, 1024), (COL_SH, 3072), (COL_SH + 3072, 128), (COL_GB, 1024)]:
        parts.append(_stat_blocks(W, c0, n))
    parts.append(_mov_blocks(W[:, COL_V:COL_V + 1024]))
    parts.append(_mov_blocks(inp["ab_w_out"][0]))
    parts.append(_mov_blocks(inp["ple_w_gate"][0]))
    parts.append(_mov_blocks(inp["c_w_in"][0]))
    parts.append(_mov_blocks(inp["c_w_out"][0]))
    parts.append(_mov_blocks(inp["ple_w_gate"][1]))
    parts.append(_mov_blocks(inp["ple_w_proj"][0]))
    parts.append(_mov_blocks(inp["ple_w_proj"][1]))
    parts.append(np.concatenate([inp["rwkv_w2"][0], inp["rwkv_a2"][0]], axis=0))
    parts.append(W[:, COL_F:COL_F + 16].reshape(KC, 128, 16).transpose(1, 0, 2).reshape(128, KC * 16))
    wall = np.concatenate(parts, axis=1)
    assert wall.shape[1] == OFF_END, (wall.shape, OFF_END)
    wall = np.concatenate([wall, np.zeros((128, WT - OFF_END), np.float32)], axis=1)
    pp = np.zeros((128, NPP), np.float32)
    pp[:, PP_MU:PP_MU + 25] = inp["rwkv_mu"][0].reshape(25, 128).T
    for off, key in [(PP_W0, "rwkv_w0"), (PP_A0, "rwkv_a0"), (PP_KK, "rwkv_k_k"), (PP_KA, "rwkv_k_a"),
                     (PP_LNG, "rwkv_ln_g"), (PP_LNB, "rwkv_ln_b")]:
        pp[:, off:off + 8] = inp[key][0].reshape(8, 128).T
    pp[:, PP_RK:PP_RK + 8] = inp["rwkv_r_k"][0].reshape(8, 128).T
    pp[:, PP_BS:PP_BS + 16] = inp["c_b_s"][0].T
    cst = np.zeros((128, NCS), np.float32)
    i = np.arange(128)
    cst[:, CS_ID:CS_ID + 128] = np.eye(128)
    cst[:, CS_SU:CS_SU + 128] = (i[:, None] < i[None, :])
    cst[:, CS_UI:CS_UI + 128] = (i[:, None] <= i[None, :])
    cst[:, CS_SL:CS_SL + 128] = (i[:, None] > i[None, :])
    cst[:, CS_B64:CS_B64 + 128] = ((i[:, None] // 64) == (i[None, :] // 64))
    cm = np.ones(1024, np.float32)
    cm[0::128] = 0.0
    cst[:, CS_CM:CS_CM + 1024] = cm[None, :]
    wsT = np.ascontiguousarray(inp["c_w_s"][0].transpose(2, 0, 1)).reshape(128, 16 * 128)
    vec = np.concatenate([inp["norm_pre"][0], inp["norm_post"][0], inp["norm_pre"][1], inp["norm_post"][1],
                          inp["c_ln_g"][0], inp["c_ln_b"][0], np.tile(inp["fox_f_bias"][0], 128)]).astype(np.float32)
    return dict(wall=np.ascontiguousarray(wall), pp=pp, cst=cst, wsT=np.ascontiguousarray(wsT), vec=vec)


V_GPRE0, V_GPOST0, V_GPRE1, V_GPOST1, V_LNG, V_LNB, V_FB = 0, 2048, 4096, 6144, 8192, 10240, 12288
NVEC = 12288 + 2048


def wkeys(off, ln):
    return [("wb", c) for c in range(off // 2048, (off + ln - 1) // 2048 + 1)]


def build(NS=2, TT=256, debug=False, phases="WABCD"):
    nc = bass.Bass("TRN2", target_bir_lowering=False)
    x = nc.dram_tensor("x", [NS, T, D], F32, kind="ExternalInput").ap()
    pin = nc.dram_tensor("p", [2, NS, T, PLE], F32, kind="ExternalInput").ap()
    wall = nc.dram_tensor("wall", [128, WT], F32, kind="ExternalInput").ap()
    ppd = nc.dram_tensor("pp", [128, NPP], F32, kind="ExternalInput").ap()
    cstd = nc.dram_tensor("cst", [128, NCS], F32, kind="ExternalInput").ap()
    wsTd = nc.dram_tensor("wsT", [128, 2048], F32, kind="ExternalInput").ap()
    vecd = nc.dram_tensor("vec", [NVEC], F32, kind="ExternalInput").ap()
    out = nc.dram_tensor("out", [NS, T, D], F32, kind="ExternalOutput").ap()
    wb = nc.dram_tensor("wb", [128, WT], BF16, kind="ExternalOutput" if debug else "Internal").ap()
    G = nc.dram_tensor("G", [NS, D, T], BF16, kind="ExternalOutput" if debug else "Internal").ap()

    P = Prog(nc)
    cst = P.sbuf([128, NCS], F32, "cst")
    pp = P.sbuf([128, NPP], F32, "pp")
    idb = P.sbuf([128, 128], BF16, "idb")
    mk2b = P.sbuf([128, 256], BF16, "mk2b")
    uib = P.sbuf([128, 128], BF16, "uib")
    P.dma("sp", cst[:], cstd, w=["cst"])
    P.dma("sp", pp[:], ppd, w=["pp"])
    P.dvei("tensor_copy", A(out=idb[:], in_=cst[:, CS_ID:CS_ID + 128]), r=["cst"], w=["idb"])
    P.dvei("tensor_copy", A(out=mk2b[:], in_=cst[:, CS_SU:CS_SU + 256]), r=["cst"], w=["mk2b"])
    P.dvei("tensor_copy", A(out=uib[:], in_=cst[:, CS_UI:CS_UI + 128]), r=["cst"], w=["uib"])
    idf = cst[:, CS_ID:CS_ID + 128]
    su_f = cst[:, CS_SU:CS_SU + 128]
    sl_f = cst[:, CS_SL:CS_SL + 128]
    b64_f = cst[:, CS_B64:CS_B64 + 128]

    rr = [0]

    def alt(*engs):
        rr[0] += 1
        return engs[rr[0] % len(engs)]

    if "W" in phases:
        with P.scope():
            NB = 4
            st = [P.sbuf([128, 2048], F32, "wst") for _ in range(NB)]
            bt = [P.sbuf([128, 2048], BF16, "wbt") for _ in range(NB)]
            nch = WT // 2048
            for c in range(nch):
                b = c % NB
                P.dma("sp", st[b][:], wall[:, c * 2048:(c + 1) * 2048], w=[("wst", b)])
                if c % 2 == 0:
                    P.dvei("tensor_copy", A(out=bt[b][:], in_=st[b][:]), r=[("wst", b)], w=[("wbt", b)])
                else:
                    P.acti("activation", A(out=bt[b][:], in_=st[b][:], func=AF.Copy), r=[("wst", b)], w=[("wbt", b)])
                P.dma("pool", wb[:, c * 2048:(c + 1) * 2048], bt[b][:], r=[("wbt", b)], w=[("wb", c)])
        P.barrier()

    def load_w(q, tile_ap, off, ln, wkey):
        return P.dma(q, tile_ap, wb[:, off:off + ln], r=wkeys(off, ln), w=[wkey])

    for s in range(NS):
        with P.scope():
            xnT = P.sbuf([128, KC * T], BF16, "xnT")
            if "A" in phases:
                with P.scope():
                    gbc = P.sbuf([128, D], F32, "gbc")
                    P.dma("act", gbc[:], vecd[V_GPRE0:V_GPRE0 + D].partition_broadcast(128), w=["gbc"])
                    xt = [P.sbuf([128, D], F32, "xt") for _ in range(2)]
                    junk = P.sbuf([128, D], BF16, "junk")
                    xnb = [P.sbuf([128, D], BF16, "xnb") for _ in range(2)]
                    ss = [P.sbuf([128, 1], F32, "ss") for _ in range(2)]
                    pT = [P.psum([128, 1024], BF16, "pT") for _ in range(2)]
                    for tt in range(NT):
                        b = tt % 2
                        P.dma("sp", xt[b][:], x[s, tt * 128:(tt + 1) * 128, :], w=[("xt", b)])
                        P.acti("activation", A(out=junk[:], in_=xt[b][:], func=AF.Square, accum_out=ss[b][:]),
                              r=[("xt", b)], w=["junk", ("ss", b)])
                        P.dvei("tensor_scalar", A(out=ss[b][:], in0=ss[b][:], scalar1=1.0 / D, scalar2=RMS_EPS,
                                                             op0=ALU.mult, op1=ALU.add), r=[("ss", b)], w=[("ss", b)])
                        P.acti("activation", A(out=ss[b][:], in_=ss[b][:], func=AF.Sqrt), r=[("ss", b)], w=[("ss", b)])
                        P.dvei("reciprocal", A(out=ss[b][:], in_=ss[b][:]), r=[("ss", b)], w=[("ss", b)])
                        P.dvei("scalar_tensor_tensor", A(out=xnb[b][:], in0=xt[b][:], scalar=ss[b][:], in1=gbc[:],
                                                                    op0=ALU.mult, op1=ALU.mult),
                              r=[("xt", b), ("ss", b), "gbc"], w=[("xnb", b)])
                        for half in range(2):
                            for j in range(8):
                                kc = half * 8 + j
                                P.pei("transpose", A(
                                    pT[half][:, j * 128:(j + 1) * 128], xnb[b][:, kc * 128:(kc + 1) * 128], idb[:]),
                                    r=[("xnb", b), "idb"], w=[("pT", half)])
                            dst = xnT[:, half * 8 * T:(half + 1) * 8 * T].rearrange("p (k t) -> p k t", k=8)[:, :, tt * 128:(tt + 1) * 128]
                            src = pT[half][:, :].rearrange("p (k t) -> p k t", k=8)
                            if half == 0:
                                P.acti("activation", A(out=dst, in_=src, func=AF.Copy),
                                      r=[("pT", half)], w=[("xnT", tt, half)])
                            else:
                                P.dvei("tensor_copy", A(out=dst, in_=src),
                                      r=[("pT", half)], w=[("xnT", tt, half)])
                P.barrier()

            def xcol(kc, t0, n):
                return xnT[:, kc * T + t0: kc * T + t0 + n]

            if "B" in phases:
                with P.scope():
                    phase_fox(P, nc, s, xnT, xcol, cst, pp, idb, uib, vecd, load_w, G, idf)
                P.barrier()
            if "C" in phases:
                with P.scope():
                    phase_rwkv(P, nc, s, xnT, xcol, cst, pp, idb, mk2b, load_w, G, idf, su_f, sl_f, b64_f)
                P.barrier()

    if "D" in phases:
        with P.scope():
            phase_tail(P, nc, NS, TT, x, pin, out, G, cst, pp, idb, vecd, wsTd, load_w, uib)
    P.emit()
    P.close()
    return nc


def phase_fox(P, nc, s, xnT, xcol, cst, pp, idb, uib, vecd, load_w, G, idf):
    vaug = P.sbuf([128, NT * 16 * 65], BF16, "vaug")
    cn = P.sbuf([128, NT * 16], F32, "cn")
    onesf = P.sbuf([128, 128], F32, "onesf")
    hi = P.sbuf([16, T], BF16, "hi")
    mid = P.sbuf([16, T], BF16, "mid")
    lo = P.sbuf([16, T], BF16, "lo")
    b0scope = P.scope()
    b0scope.__enter__()
    wv = [P.sbuf([128, KC * 512], BF16, "wv") for _ in range(2)]
    wf = P.sbuf([128, KC * 16], BF16, "wf")
    nl = P.sbuf([128, NT * 16], F32, "nl")
    rsum = P.sbuf([128, 16], F32, "rsum")
    fbc = P.sbuf([128, 16], F32, "fbc")
    cnT = P.sbuf([16, T], F32, "cnT")
    r1 = P.sbuf([16, T], F32, "r1")
    load_w("sp", wv[0][:], OFF_WV, 8192, "wv0")
    load_w("act", wv[1][:], OFF_WV + 8192, 8192, "wv1")
    load_w("sp", wf[:], OFF_WF, 256, "wf")
    P.dma("act", fbc[:], vecd[V_FB:V_FB + 2048].rearrange("(p h) -> p h", h=16), w=["fbc"])
    P.pooli("memset", A(onesf[:], 1.0), w=["onesf"])
    P.pooli("memset", A(rsum[:], 0.0), w=["rsum"])
    va4 = vaug[:, :].rearrange("p (t h d) -> p t h d", t=NT, h=16)
    P.pooli("memset", A(vaug[:, :].rearrange("p (n d) -> p n d", d=65)[:, :, 64:65], 1.0), w=["vones"])
    pv = [P.psum([128, 512], F32, "pv") for _ in range(2)]
    pf = P.psum([128, 512], F32, "pf")
    for tt in range(NT):
        for kc in range(KC):
            lt = xcol(kc, tt * 128, 128)
            for j in range(2):
                P.pei("matmul", A(pv[j][:], lt, wv[j][:, kc * 512:(kc + 1) * 512],
                                                            start=(kc == 0), stop=(kc == KC - 1)), r=[f"wv{j}"], w=[("pv", j)])
            P.pei("matmul", A(pf[:, 0:16], lt, wf[:, kc * 16:(kc + 1) * 16],
                                                  start=(kc == 0), stop=(kc == KC - 1)), r=["wf"], w=["pf"])
        P.acti("activation", A(out=va4[:, tt, 0:8, 0:64], in_=pv[0][:, :].rearrange("p (h d) -> p h d", h=8), func=AF.Copy),
              r=[("pv", 0)], w=[("vaug", tt, 0)])
        P.dvei("tensor_copy", A(out=va4[:, tt, 8:16, 0:64], in_=pv[1][:, :].rearrange("p (h d) -> p h d", h=8)),
              r=[("pv", 1)], w=[("vaug", tt, 1)])
        nls = nl[:, tt * 16:(tt + 1) * 16]
        P.dvei("tensor_tensor", A(out=nls, in0=pf[:, 0:16], in1=fbc[:], op=ALU.add), r=["pf", "fbc"], w=[("nl", tt)])
        P.acti("activation", A(out=nls, in_=nls, func=AF.Exp, scale=-1.0), r=[("nl", tt)], w=[("nl", tt)])
        P.acti("activation", A(out=nls, in_=nls, func=AF.Ln, bias=1.0), r=[("nl", tt)], w=[("nl", tt)])
    pc = P.psum([128, 512], F32, "pc")
    ptr = P.psum([128, 512], F32, "ptr")
    ui_f = cst[:, CS_UI:CS_UI + 128]
    for tt in range(NT):
        nls = nl[:, tt * 16:(tt + 1) * 16]
        cs = cn[:, tt * 16:(tt + 1) * 16]
        P.pei("matmul", A(pc[:, 0:16], ui_f, nls, start=True, stop=False), r=[("nl", tt), "cst"], w=["pc"])
        P.pei("matmul", A(pc[:, 0:16], onesf[:], rsum[:], start=False, stop=True), r=["onesf", "rsum"], w=["pc"])
        P.acti("activation", A(out=cs, in_=pc[:, 0:16], func=AF.Copy), r=["pc"], w=[("cn", tt)])
        P.dvei("tensor_tensor", A(out=rsum[:], in0=rsum[:], in1=nls, op=ALU.add), r=["rsum", ("nl", tt)], w=["rsum"])
        P.pei("transpose", A(ptr[0:16, 0:128], cs, idf), r=[("cn", tt), "cst"], w=["ptr"])
        P.dvei("tensor_single_scalar", A(out=cnT[:, tt * 128:(tt + 1) * 128], in_=ptr[0:16, 0:128], scalar=-1.0, op=ALU.mult),
              r=["ptr"], w=["cnT"])
    P.dvei("tensor_copy", A(out=hi[:], in_=cnT[:]), r=["cnT"], w=["hi"])
    P.dvei("tensor_tensor", A(out=r1[:], in0=cnT[:], in1=hi[:], op=ALU.subtract), r=["cnT", "hi"], w=["r1"])
    P.dvei("tensor_copy", A(out=mid[:], in_=r1[:]), r=["r1"], w=["mid"])
    P.dvei("tensor_tensor", A(out=r1[:], in0=r1[:], in1=mid[:], op=ALU.subtract), r=["r1", "mid"], w=["r1"])
    P.dvei("tensor_copy", A(out=lo[:], in_=r1[:]), r=["r1"], w=["lo"])

    b0scope.__exit__(None, None, None)
    P.barrier()
    wq = P.sbuf([128, 2048], BF16, "wq")
    wk = P.sbuf([128, 2048], BF16, "wk")
    wg = P.sbuf([128, 2048], BF16, "wg")
    QA = [P.sbuf([67, T], BF16, "QA") for _ in range(2)]
    KA = [P.sbuf([67, T], BF16, "KA") for _ in range(2)]
    sg = P.sbuf([128, T], BF16, "sg")
    go = P.sbuf([128, T], BF16, "go")
    NPT = 4
    pt = [P.sbuf([128, 512], BF16, "pt") for _ in range(NPT)]
    rrow = P.sbuf([65, 512], F32, "rrow")
    rb = P.sbuf([64, 512], F32, "rb")
    tmpo = P.sbuf([128, 512], F32, "tmpo")
    pp_ = [P.psum([128, 512], F32, "pproj") for _ in range(2)]
    ps_ = [P.psum([128, 512], F32, "pst") for _ in range(2)]
    po_ = [P.psum([128, 512], F32, "po") for _ in range(2)]
    pbk = P.psum([128, 512], F32, "pbk")
    for i in range(2):
        P.pooli("memset", A(KA[i][64:67, :], 1.0), w=[("KAones", i)])
    pti = [0]
    for hp in range(8):
        load_w("sp", wq[:], OFF_STAT + (BQ + hp) * 2048, 2048, "wq")
        load_w("act", wk[:], OFF_STAT + (BK + hp) * 2048, 2048, "wk")
        load_w("sp", wg[:], OFF_STAT + (BG + hp) * 2048, 2048, "wg")
        for i in range(2):
            h = hp * 2 + i
            for (src, row) in ((hi, 64), (mid, 65), (lo, 66)):
                P.dma("pool", QA[i][row:row + 1, :], src[h:h + 1, :], r=["hi", "mid", "lo"], w=[("QAaug", i)])
        pi = 0
        for tq in range(4):
            cols = slice(tq * 512, (tq + 1) * 512)
            for which, wt in (("q", wq), ("k", wk), ("g", wg)):
                ps = pp_[pi % 2]
                pk = ("pproj", pi % 2)
                pi += 1
                for kc in range(KC):
                    P.pei("matmul", A(ps[:], wt[:, kc * 128:(kc + 1) * 128], xcol(kc, tq * 512, 512),
                                                                       start=(kc == 0), stop=(kc == KC - 1)), r=["w" + which], w=[pk])
                if which == "q":
                    P.acti("activation", A(out=QA[0][0:64, cols], in_=ps[0:64, :], func=AF.Copy, scale=0.125),
                          r=[pk], w=[("QA", 0, tq)])
                    P.dvei("tensor_single_scalar", A(out=QA[1][0:64, cols], in_=ps[64:128, :], scalar=0.125, op=ALU.mult),
                          r=[pk], w=[("QA", 1, tq)])
                elif which == "k":
                    P.acti("activation", A(out=KA[0][0:64, cols], in_=ps[0:64, :], func=AF.Copy),
                          r=[pk], w=[("KA", 0, tq)])
                    P.dvei("tensor_copy", A(out=KA[1][0:64, cols], in_=ps[64:128, :]), r=[pk], w=[("KA", 1, tq)])
                else:
                    P.acti("activation", A(out=sg[:, cols], in_=ps[:], func=AF.Silu), r=[pk], w=[("sg", tq)])
        items = [(i, qt, kb) for i in range(2) for qt in range(4) for kb in range(4 * qt + 4)]
        info = {}

        def emit_S(n):
            i, qt, kb = items[n]
            q0, k0 = qt * 512, kb * 128
            c0 = max(0, k0 - q0)
            nq = 512 - c0
            ps = ps_[n % 2]
            psk = ("pst", n % 2)
            kr = [("KA", i, k0 // 512), ("KAones", i), ("QA", i, qt), ("QAaug", i)]
            P.pei("matmul", A(ps[:, 0:nq], KA[i][0:67, k0:k0 + 128], QA[i][0:67, q0 + c0:q0 + 512], start=True, stop=True), r=kr, w=[psk])
            info[n] = (ps, psk, c0, nq)

        def emit_rest(n):
            i, qt, kb = items[n]
            h = hp * 2 + i
            q0, k0 = qt * 512, kb * 128
            nkb = 4 * qt + 4
            ps, psk, c0, nq = info.pop(n)
            po = po_[qt % 2]
            pok = ("po", qt % 2)
            ptile = pt[pti[0] % NPT]
            ptk = ("pt", pti[0] % NPT)
            pti[0] += 1
            bias = cn[:, kb * 16 + h: kb * 16 + h + 1]
            P.acti("activation", A(out=ptile[:, 0:nq], in_=ps[:, 0:nq], func=AF.Exp, bias=bias), r=[psk, ("cn", kb)], w=[ptk])
            if k0 >= q0:
                P.pooli("tensor_tensor", A(out=ptile[:, 0:128], in0=ptile[:, 0:128], in1=uib[:], op=ALU.mult), r=[ptk, "uib"], w=[ptk])
            P.pei("matmul", A(po[0:65, c0:512], va4[:, kb, h, :], ptile[:, 0:nq], start=(kb == 0), stop=(kb == nkb - 1)),
                  r=[ptk, ("vaug", kb, h // 8), "vones"], w=[pok])

        def emit_final(i, qt):
            q0 = qt * 512
            po = po_[qt % 2]
            pok = ("po", qt % 2)
            P.dvei("reciprocal", A(out=rrow[64:65, :], in_=po[64:65, :]), r=[pok], w=["rrow"])
            P.pei("matmul", A(pbk[0:64, :], onesf[64:65, 0:64], rrow[64:65, :], start=True, stop=True), r=["rrow", "onesf"], w=["pbk"])
            P.acti("activation", A(out=rb[:], in_=pbk[0:64, :], func=AF.Copy), r=["pbk"], w=["rb"])
            P.dvei("tensor_tensor", A(out=tmpo[i * 64:(i + 1) * 64, :], in0=po[0:64, :], in1=rb[:], op=ALU.mult), r=[pok, "rb"], w=["tmpo"])
            P.pooli("tensor_tensor", A(out=go[i * 64:(i + 1) * 64, q0:q0 + 512], in0=tmpo[i * 64:(i + 1) * 64, :],
                                       in1=sg[i * 64:(i + 1) * 64, q0:q0 + 512], op=ALU.mult), r=["tmpo", ("sg", qt)], w=[("go", i, qt)])

        pending = []
        emit_S(0)
        for n in range(len(items)):
            if n + 1 < len(items):
                emit_S(n + 1)
            emit_rest(n)
            i, qt, kb = items[n]
            pending = [(d - 1, a, b) for (d, a, b) in pending]
            while pending and pending[0][0] <= 0:
                _, a, b = pending.pop(0)
                emit_final(a, b)
            if kb == 4 * qt + 3:
                pending.append((2, i, qt))
        for _, a, b in pending:
            emit_final(a, b)
        P.dma("sp", G[s, hp * 128:(hp + 1) * 128, :], go[:], r=[("go", i, qt) for i in range(2) for qt in range(4)], w=[("G", s, hp)])


class PSlots:
    def __init__(self, P, nbanks, name):
        self.banks = [P.psum([128, 512], F32, name) for _ in range(nbanks)]
        self.name = name
        self.cur = 0
        self.nb = nbanks

    def get(self, ncols):
        b = self.cur % self.nb
        self.cur += 1
        q = (ncols + 127) // 128
        return self.banks[b], 0, [(self.name, b)] * q


import os
K_STOP = int(os.environ.get('K_STOP', '99'))
K_SUB = int(os.environ.get('K_SUB', '255'))


def phase_rwkv(P, nc, s, xnT, xcol, cst, pp, idb, mk2b, load_w, G, idf, su_f, sl_f, b64_f):
    TH = 1024
    NH = T // TH
    NCk = TH // 128
    NTQ = TH // 512
    cm = cst[:, CS_CM:CS_CM + TH]
    uib_ = mk2b[:, 128:256]
    LO = P.sbuf([128, T], BF16, "LO")
    LW = P.sbuf([128, 1024], BF16, "LW")
    load_w("act", LW[:], OFF_LORA, 1024, "LW")
    raw = {n: P.sbuf([128, TH + 1], F32, "raw" + n) for n in "RKV"}
    carry = {n: P.sbuf([128, 1], F32, "carry" + n) for n in "RKV"}
    F = [P.sbuf([128, TH], F32, f"F{i}") for i in range(6)]
    KR = P.sbuf([128, NCk * 256], BF16, "KR")
    KH = P.sbuf([128, TH], BF16, "KH")
    NB = P.sbuf([128, TH], BF16, "NB")
    VB = P.sbuf([128, TH], BF16, "VB")
    sgb = P.sbuf([128, TH], BF16, "sgb")
    TM = P.sbuf([128, NCk * 512], BF16, "TM")
    YT = P.sbuf([128, TH], F32, "YT")
    go = P.sbuf([128, T], BF16, "gor")
    gcs = P.sbuf([64, 2 * NCk], F32, "gcs")
    Zs = [[P.sbuf([64, 64], BF16, "Zs") for _ in range(2)] for _ in range(2)]
    wts = {n: P.sbuf([128, 2048], BF16, "w" + n) for n in ("R", "K", "V", "G")}
    NE = 8
    Qb = [P.sbuf([128, NE * 128], BF16, "Qb") for _ in range(2)]
    QTb = [P.sbuf([128, NE * 128], BF16, "QTb") for _ in range(2)]
    PPb = [P.sbuf([128, NE * 128], F32, "PPb") for _ in range(2)]
    PPh = [P.sbuf([128, NE * 128], BF16, "PPh") for _ in range(2)]
    tn1 = [P.sbuf([128, 256], F32, "tn1") for _ in range(2)]
    LA = P.sbuf([128, NE * 256], BF16, "LA")
    nArb = P.sbuf([128, NE * 128], BF16, "nArb")
    TTb = P.sbuf([128, NE * 128], BF16, "TTb")
    LkVs = P.sbuf([128, NE * 64], BF16, "LkVs")
    KUs = P.sbuf([128, NE * 128], BF16, "KUs")
    RpTs = P.sbuf([64, NE * 128], BF16, "RpTs")
    MpTs = P.sbuf([64, NE * 64], BF16, "MpTs")
    ptrs = [P.psum([128, 1024], BF16, "ptrb") for _ in range(2)]
    PS = PSlots(P, 6, "psr")

    def FK(i):
        return [(f"F{i}", q) for q in range(NTQ)]

    def RK(n):
        return [("raw" + n, q) for q in range(NTQ)]

    def evac_copy(eng, out, in_, r, w):
        if eng == "act":
            P.acti("activation", A(out=out, in_=in_, func=AF.Copy), r=r, w=w)
        else:
            P.dvei("tensor_copy", A(out=out, in_=in_), r=r, w=w)

    def project(wt, wkey, t0, sink):
        for tq in range(NTQ):
            ps, c0, keys = PS.get(512)
            for kc in range(KC):
                P.pei("matmul", A(ps[:, :], wt[:, kc * 128:(kc + 1) * 128], xcol(kc, t0 + tq * 512, 512),
                                                            start=(kc == 0), stop=(kc == KC - 1)), r=[wkey], w=keys)
            sink(tq, ps, keys)

    def mix(n, mucol, hf):
        rw = raw[n]
        rk_ = [("raw" + n, q) for q in range(NTQ)]
        P.dvei("tensor_copy", A(out=carry[n][:], in_=rw[:, TH:TH + 1]), r=rk_ + [("raw0" + n)], w=["carry" + n])
        P.dvei("tensor_tensor", A(out=F[0][:], in0=rw[:, 0:TH], in1=rw[:, 1:TH + 1], op=ALU.subtract),
              r=rk_ + ["raw0" + n], w=[*FK(0)])
        P.dvei("scalar_tensor_tensor", A(out=rw[:, 1:TH + 1], in0=F[0][:], scalar=pp[:, PP_MU + mucol:PP_MU + mucol + 1],
                                               in1=rw[:, 1:TH + 1], op0=ALU.mult, op1=ALU.add), r=[*FK(0), "pp"] + rk_, w=rk_)

    def set_prev(n, hf):
        rw = raw[n]
        if hf == 0:
            P.pooli("memset", A(rw[:, 0:1], 0.0), w=["raw0" + n])
        else:
            P.pooli("tensor_copy", A(out=rw[:, 0:1], in_=carry[n][:]), r=["carry" + n], w=["raw0" + n])

    load_w("sp", wts["R"][:], OFF_STAT + BLO * 2048, 2048, "wR")
    for hf in range(NH):
        t0 = hf * TH
        set_prev("R", hf)

        def sink(tq, ps, keys):
            evac_copy("act" if tq % 2 == 0 else "dve", raw["R"][:, 1 + tq * 512:1 + (tq + 1) * 512], ps[:, :], keys, [("rawR", tq)])
        project(wts["R"], "wR", t0, sink)
        mix("R", 24, hf)
        P.acti("activation", A(out=LO[0:64, t0:t0 + TH], in_=raw["R"][0:64, 1:TH + 1], func=AF.Tanh), r=RK("R"), w=[("LO", hf, 0)])
        P.dvei("tensor_copy", A(out=LO[64:128, t0:t0 + TH], in_=raw["R"][64:128, 1:TH + 1]), r=RK("R"), w=[("LO", hf, 1)])

    if K_STOP == 0:
        return
    def fk(i):
        return [f"F{i}"]

    for hp in range(8):
        for n, blk in (("R", BR), ("K", BKB), ("V", BVB), ("G", BGB)):
            load_w("sp" if n in "RV" else "act", wts[n][:], OFF_STAT + (blk + hp) * 2048, 2048, "w" + n)
        for i in range(2):
            P.pooli("memset", A(Zs[i][0][:], 0.0), w=[("Z", i, 0)])
        for hf in range(NH):
            t0 = hf * TH
            for n in "RKV":
                set_prev(n, hf)

                def sink(tq, ps, keys, n=n):
                    evac_copy("act" if tq % 2 == 0 else "dve", raw[n][:, 1 + tq * 512:1 + (tq + 1) * 512], ps[:, :],
                              keys, [("raw" + n, tq)])
                project(wts[n], "w" + n, t0, sink)

            def sinkg(tq, ps, keys):
                P.acti("activation", A(out=sgb[:, tq * 512:(tq + 1) * 512], in_=ps[:, :], func=AF.Silu), r=keys, w=[("sgb", tq)])
            project(wts["G"], "wG", t0, sinkg)
            for n, mc in (("R", hp), ("K", 8 + hp), ("V", 16 + hp)):
                mix(n, mc, hf)
            rR, rK, rV = raw["R"][:, 1:TH + 1], raw["K"][:, 1:TH + 1], raw["V"][:, 1:TH + 1]
            if K_STOP == 1:
                return
            for tq in range(NTQ):
                cols = slice(tq * 512, (tq + 1) * 512)
                ps, c0, keys = PS.get(512)
                P.pei("matmul", A(ps[:, :], LW[0:64, hp * 128:(hp + 1) * 128], LO[0:64, t0 + tq * 512:t0 + (tq + 1) * 512],
                                                      start=True, stop=True), r=["LW", ("LO", hf, 0)], w=keys)
                P.acti("activation", A(out=F[0][:, cols], in_=ps[:, :], func=AF.Sigmoid,
                                                                bias=pp[:, PP_W0 + hp:PP_W0 + hp + 1]), r=keys + ["pp"], w=[("F0", tq)])
                ps2, c0, keys2 = PS.get(512)
                P.pei("matmul", A(ps2[:, :], LW[64:128, hp * 128:(hp + 1) * 128], LO[64:128, t0 + tq * 512:t0 + (tq + 1) * 512],
                                                        start=True, stop=True), r=["LW", ("LO", hf, 1)], w=keys2)
                P.acti("activation", A(out=F[1][:, cols], in_=ps2[:, :], func=AF.Sigmoid,
                                                                 bias=pp[:, PP_A0 + hp:PP_A0 + hp + 1]), r=keys2 + ["pp"], w=[("F1", tq)])
            F0k = [("F0", q) for q in range(NTQ)]
            F1k = [("F1", q) for q in range(NTQ)]
            P.dvei("tensor_tensor_scan", A(out=F[2][:], data0=cm, data1=F[0][:], initial=0.0, op0=ALU.mult, op1=ALU.add),
                  r=F0k + ["cst"], w=[*FK(2)])
            P.dvei("tensor_tensor", A(out=F[0][:], in0=F[2][:], in1=F[0][:], op=ALU.subtract), r=[*FK(2)] + F0k, w=[*FK(0)])
            P.acti("activation", A(out=F[0][:], in_=F[0][:], func=AF.Exp, scale=-C0), r=[*FK(0)], w=[*FK(0)])
            P.acti("activation", A(out=F[3][:], in_=F[2][:], func=AF.Exp, scale=-C0), r=[*FK(2)], w=[*FK(3)])
            P.acti("activation", A(out=F[2][:], in_=F[2][:], func=AF.Exp, scale=C0), r=[*FK(2)], w=[*FK(2)])
            if K_STOP == 2:
                return
            P.dvei("tensor_single_scalar", A(out=F[4][:], in_=rK, scalar=pp[:, PP_KK + hp:PP_KK + hp + 1], op=ALU.mult),
                  r=RK("K") + ["pp"], w=[*FK(4)])
            P.pooli("tensor_tensor", A(out=F[5][:], in0=F[4][:], in1=F[4][:], op=ALU.mult), r=[*FK(4)], w=[*FK(5)])
            for tq in range(NTQ):
                cols = slice(tq * 512, (tq + 1) * 512)
                ps, c0, keys = PS.get(512)
                P.pei("matmul", A(ps[:, :], b64_f, F[5][:, cols], start=True, stop=True), r=[*FK(5), "cst"], w=keys)
                P.dvei("tensor_single_scalar", A(out=F[5][:, cols], in_=ps[:, :], scalar=1e-24, op=ALU.max),
                      r=keys, w=[("F5", tq)])
            F5k = [("F5", q) for q in range(NTQ)]
            P.acti("activation", A(out=F[5][:], in_=F[5][:], func=AF.Sqrt), r=F5k, w=[*FK(5)])
            P.dvei("reciprocal", A(out=F[5][:], in_=F[5][:]), r=[*FK(5)], w=[*FK(5)])
            P.dvei("tensor_tensor", A(out=F[4][:], in0=F[4][:], in1=F[5][:], op=ALU.mult), r=[*FK(4), *FK(5)], w=[*FK(4)])
            KR3 = KR[:, :].rearrange("p (c x) -> p c x", x=256)
            v3 = lambda ap: ap.rearrange("p (c x) -> p c x", x=128)
            P.pooli("tensor_tensor", A(out=KR3[:, :, 0:128], in0=v3(F[4][:, :]), in1=v3(F[0][:, :]), op=ALU.mult),
                   r=[*FK(4), *FK(0)], w=["KRk"])
            P.dvei("tensor_tensor", A(out=F[0][:], in0=F[4][:], in1=F[1][:], op=ALU.mult), r=[*FK(4), "KRk"] + F1k, w=[*FK(0)])
            P.dvei("scalar_tensor_tensor", A(out=NB[:], in0=F[0][:], scalar=-1.0, in1=F[2][:], op0=ALU.mult, op1=ALU.mult),
                  r=[*FK(0), *FK(2)], w=["NB"])
            P.pooli("tensor_scalar", A(out=F[0][:], in0=F[1][:], scalar1=-1.0, scalar2=pp[:, PP_KA + hp:PP_KA + hp + 1],
                                             op0=ALU.add, op1=ALU.mult), r=F1k + ["NB", "pp"], w=[*FK(0)])
            P.dvei("scalar_tensor_tensor", A(out=F[0][:], in0=F[0][:], scalar=1.0, in1=rK, op0=ALU.add, op1=ALU.mult),
                   r=[*FK(0)] + RK("K"), w=[*FK(0)])
            P.dvei("tensor_tensor", A(out=KH[:], in0=F[0][:], in1=F[2][:], op=ALU.mult), r=[*FK(0), *FK(2)], w=["KH"])
            P.pooli("tensor_tensor", A(out=KR3[:, :, 128:256], in0=v3(rR), in1=v3(F[3][:, :]), op=ALU.mult),
                   r=RK("R") + [*FK(3)], w=["KRr"])
            g3 = v3(F[3][:, :])
            P.acti("activation", A(out=gcs[:, 0:NCk], in_=g3[0:64, :, 127], func=AF.Copy), r=[*FK(3)], w=["gcs"])
            P.acti("activation", A(out=gcs[:, NCk:2 * NCk], in_=g3[64:128, :, 127], func=AF.Copy), r=[*FK(3)], w=["gcs"])
            P.dvei("scalar_tensor_tensor", A(out=F[1][:], in0=rR, scalar=pp[:, PP_RK + hp:PP_RK + hp + 1], in1=F[0][:],
                                                   op0=ALU.mult, op1=ALU.mult), r=RK("R") + [*FK(0), "pp"] + F1k, w=[*FK(1)])
            P.acti("activation", A(out=VB[:], in_=rV, func=AF.Copy), r=RK("V"), w=["VB"])
            if K_STOP == 3:
                return
            for c in range(NCk):
                half = c % 2
                ptr = ptrs[half]
                for k_, (src, key) in enumerate(((KR[:, c * 256:c * 256 + 128], "KRk"), (VB[:, c * 128:(c + 1) * 128], "VB"),
                                                 (KH[:, c * 128:(c + 1) * 128], "KH"), (NB[:, c * 128:(c + 1) * 128], "NB"))):
                    P.pei("transpose", A(ptr[:, k_ * 128:(k_ + 1) * 128], src, idb[:]),
                         r=[key, "idb"], w=[("ptr", half)])
                evac_copy("act" if c % 2 == 0 else "dve", TM[:, c * 512:(c + 1) * 512], ptr[:, 0:512],
                          [("ptr", half)], [("TM", c)])
            if K_STOP == 4:
                return
            NBATCH = NCk // 4
            for bt in range(NBATCH):
                es = [(cl, i) for cl in range(4) for i in range(2)]
                for e_, (cl, i) in enumerate(es):
                    c = bt * 4 + cl
                    hs = slice(i * 64, (i + 1) * 64)
                    cch = slice(c * 128, (c + 1) * 128)
                    pa, a0, ka = PS.get(256)
                    pa2, a02, ka2 = PS.get(256)
                    P.pei("matmul", A(pa[:, 0:256], NB[hs, cch], KR[hs, c * 256:(c + 1) * 256], start=True, stop=True),
                          r=["NB", "KRk", "KRr"], w=ka)
                    P.pei("matmul", A(pa2[:, 0:256], KH[hs, cch], KR[hs, c * 256:(c + 1) * 256], start=True, stop=True),
                          r=["KH", "KRk", "KRr"], w=ka2)
                    p3, a3, k3 = PS.get(128)
                    P.pei("matmul", A(p3[:, a3:a3 + 128], KR[hs, c * 256:c * 256 + 128], NB[hs, cch],
                                                                                  start=True, stop=True), r=["NB", "KRk"], w=k3)
                    t1 = tn1[e_ % 2]
                    t1k = ("tn1", e_ % 2)
                    P.acti("activation", A(out=t1[:], in_=pa[:, 0:256], func=AF.Copy), r=ka, w=[t1k])
                    esl = slice(e_ * 128, (e_ + 1) * 128)
                    P.pooli("tensor_tensor", A(out=Qb[0][:, esl], in0=t1[:, 0:128], in1=su_f, op=ALU.mult),
                               r=[t1k, "cst"], w=[("Q", 0, e_)])
                    P.pooli("tensor_tensor", A(out=nArb[:, esl], in0=t1[:, 128:256], in1=cst[:, CS_UI:CS_UI + 128], op=ALU.mult),
                               r=[t1k, "cst"], w=[("nArb", e_)])
                    P.dvei("tensor_tensor", A(out=LA[:, e_ * 256:(e_ + 1) * 256], in0=pa2[:, 0:256], in1=cst[:, CS_SU:CS_SU + 256], op=ALU.mult),
                              r=ka2 + ["cst"], w=[("LA", e_)])
                    P.dvei("tensor_tensor", A(out=QTb[0][:, esl], in0=p3[:, a3:a3 + 128], in1=sl_f, op=ALU.mult),
                              r=k3 + ["cst"], w=[("QT", 0, e_)])
                    P.pooli("tensor_tensor", A(out=PPb[0][:, esl], in0=Qb[0][:, esl], in1=idf, op=ALU.add),
                               r=[("Q", 0, e_), "cst"], w=[("PP", 0, e_)])
                    P.pooli("tensor_tensor", A(out=PPh[0][:, esl], in0=Qb[0][:, esl], in1=idb[:], op=ALU.add),
                               r=[("Q", 0, e_), "idb"], w=[("PPh", 0, e_)])
                if K_STOP == 5:
                    return
                for lv in range(1, 7):
                    cur, nxt = (lv - 1) % 2, lv % 2
                    slots = []
                    for e_ in range(NE):
                        esl = slice(e_ * 128, (e_ + 1) * 128)
                        pq, q0_, kq = PS.get(128)
                        P.pei("matmul", A(pq[:, 0:128], Qb[cur][:, esl], QTb[cur][:, esl], start=True, stop=True),
                              r=[("Q", cur, e_), ("QT", cur, e_)], w=kq)
                        P.acti("activation", A(out=QTb[nxt][:, esl], in_=pq[:, 0:128], func=AF.Copy), r=kq, w=[("QT", nxt, e_)])
                        if lv < 6:
                            pq2, _, kq2 = PS.get(128)
                            P.pei("matmul", A(pq2[:, 0:128], QTb[cur][:, esl], Qb[cur][:, esl], start=True, stop=True),
                                  r=[("Q", cur, e_), ("QT", cur, e_)], w=kq2)
                            P.dvei("tensor_copy", A(out=Qb[nxt][:, esl], in_=pq2[:, 0:128]), r=kq2, w=[("Q", nxt, e_)])
                    for e_ in range(NE):
                        esl = slice(e_ * 128, (e_ + 1) * 128)
                        p1, o1, k1 = PS.get(128)
                        P.pei("matmul", A(p1[:, o1:o1 + 128], QTb[nxt][:, esl], PPh[cur][:, esl], start=True, stop=True),
                              r=[("QT", nxt, e_), ("PPh", cur, e_)], w=k1)
                        if lv < 6:
                            P.dvei("tensor_tensor", A(out=PPb[nxt][:, esl], in0=p1[:, o1:o1 + 128], in1=PPb[cur][:, esl], op=ALU.add),
                                   r=k1 + [("PP", cur, e_)], w=[("PP", nxt, e_)])
                            P.pooli("tensor_copy", A(out=PPh[nxt][:, esl], in_=PPb[nxt][:, esl]), r=[("PP", nxt, e_)], w=[("PPh", nxt, e_)])
                        else:
                            P.dvei("tensor_tensor", A(out=TTb[:, esl], in0=p1[:, o1:o1 + 128], in1=PPb[cur][:, esl], op=ALU.add),
                                   r=k1 + [("PP", cur, e_)], w=[("TT", e_)])
                for e_, (cl, i) in enumerate(es):
                    c = bt * 4 + cl
                    esl = slice(e_ * 128, (e_ + 1) * 128)
                    tmc = c * 512
                    Kt_tok = TM[:, tmc + i * 64: tmc + (i + 1) * 64]
                    V_tok = TM[:, tmc + 128 + i * 64: tmc + 128 + (i + 1) * 64]
                    NB_tok = TM[:, tmc + 384 + i * 64: tmc + 384 + (i + 1) * 64]
                    pa, a0, ka = PS.get(128)
                    P.pei("matmul", A(pa[:, a0:a0 + 64], LA[:, e_ * 256:e_ * 256 + 128], V_tok, start=True, stop=True),
                         r=[("LA", e_), ("TM", c)], w=ka)
                    P.acti("activation", A(out=LkVs[:, e_ * 64:(e_ + 1) * 64], in_=pa[:, a0:a0 + 64], func=AF.Copy),
                          r=ka, w=[("LkV", e_)])
                    pb, b0, kb = PS.get(128)
                    P.pei("matmul", A(pb[:, b0:b0 + 64], TTb[:, esl], Kt_tok, start=True, stop=True),
                         r=[("TT", e_), ("TM", c)], w=kb)
                    P.pei("matmul", A(pb[:, b0 + 64:b0 + 128], TTb[:, esl], LkVs[:, e_ * 64:(e_ + 1) * 64],
                                                                         start=True, stop=True), r=[("TT", e_), ("LkV", e_)], w=kb)
                    P.dvei("tensor_copy", A(out=KUs[:, esl], in_=pb[:, b0:b0 + 128]), r=kb, w=[("KU", e_)])
                    pc_, c0_, kc_ = PS.get(128)
                    hs = slice(i * 64, (i + 1) * 64)
                    P.pei("matmul", A(pc_[0:64, c0_:c0_ + 128], KUs[:, e_ * 128:e_ * 128 + 64], nArb[:, esl], start=True, stop=False),
                          r=[("KU", e_), ("nArb", e_)], w=kc_)
                    P.pei("matmul", A(pc_[0:64, c0_:c0_ + 128], idb[hs, hs], KR[hs, c * 256 + 128:c * 256 + 256], start=False, stop=True),
                          r=["KRr", "idb"], w=kc_)
                    P.acti("activation", A(out=RpTs[:, esl], in_=pc_[0:64, c0_:c0_ + 128], func=AF.Copy), r=kc_, w=[("RpT", e_)])
                    pd, d0, kd = PS.get(128)
                    P.pei("matmul", A(pd[0:64, d0:d0 + 64], KUs[:, e_ * 128:e_ * 128 + 64], NB_tok,
                                                                               start=True, stop=True), r=[("KU", e_), ("TM", c)], w=kd)
                    P.dvei("tensor_tensor", A(out=MpTs[:, e_ * 64:(e_ + 1) * 64], in0=pd[0:64, d0:d0 + 64],
                                                                        in1=cst[0:64, CS_ID:CS_ID + 64], op=ALU.add),
                          r=kd + ["cst"], w=[("MpT", e_)])
                if K_STOP == 7:
                    return
                for e_, (cl, i) in enumerate(es):
                    c = bt * 4 + cl
                    gc_ = hf * NCk + c
                    cur, nxt = gc_ % 2, (gc_ + 1) % 2
                    esl = slice(e_ * 128, (e_ + 1) * 128)
                    tmc = c * 512
                    V_tok = TM[:, tmc + 128 + i * 64: tmc + 128 + (i + 1) * 64]
                    KH_tok = TM[:, tmc + 256 + i * 64: tmc + 256 + (i + 1) * 64]
                    NB_tok = TM[:, tmc + 384 + i * 64: tmc + 384 + (i + 1) * 64]
                    Uv = KUs[:, e_ * 128 + 64:(e_ + 1) * 128]
                    Zc = Zs[i][cur]
                    py, y0, ky = PS.get(128)
                    P.pei("matmul", A(py[0:64, y0:y0 + 128], Zc[:, :], RpTs[:, esl], start=True, stop=False),
                         r=[("Z", i, cur), ("RpT", e_)], w=ky)
                    P.pei("matmul", A(py[0:64, y0:y0 + 128], V_tok, LA[:, e_ * 256 + 128:(e_ + 1) * 256],
                                                                             start=False, stop=False), r=[("TM", c), ("LA", e_)], w=ky)
                    P.pei("matmul", A(py[0:64, y0:y0 + 128], Uv, nArb[:, esl], start=False, stop=True),
                         r=[("KU", e_), ("nArb", e_)], w=ky)
                    P.acti("activation", A(out=YT[i * 64:(i + 1) * 64, c * 128:(c + 1) * 128], in_=py[0:64, y0:y0 + 128],
                                                                        func=AF.Copy), r=ky, w=[("YT", i, c)])
                    pz, z0, kz = PS.get(128)
                    P.pei("matmul", A(pz[0:64, z0:z0 + 64], MpTs[:, e_ * 64:(e_ + 1) * 64], Zc[:, :], start=True, stop=False),
                         r=[("Z", i, cur), ("MpT", e_)], w=kz)
                    P.pei("matmul", A(pz[0:64, z0:z0 + 64], KH_tok, V_tok, start=False, stop=False),
                         r=[("TM", c)], w=kz)
                    P.pei("matmul", A(pz[0:64, z0:z0 + 64], NB_tok, Uv, start=False, stop=True),
                         r=[("TM", c), ("KU", e_)], w=kz)
                    P.dvei("tensor_single_scalar", A(out=Zs[i][nxt][:, :], in_=pz[0:64, z0:z0 + 64],
                                                                                          scalar=gcs[:, i * NCk + c:i * NCk + c + 1], op=ALU.mult),
                          r=kz + ["gcs"], w=[("Z", i, nxt)])
            if K_STOP == 8:
                return
            YTk = [("YT", i, c) for i in range(2) for c in range(NCk)]
            for tq in range(NTQ):
                cols = slice(tq * 512, (tq + 1) * 512)
                P.acti("activation", A(out=F[2][:, cols], in_=YT[:, cols], func=AF.Square), r=YTk, w=[("F2", tq)])
                pm, _, km = PS.get(512)
                P.pei("matmul", A(pm[:, :], b64_f, YT[:, cols], start=True, stop=True), r=YTk + ["cst"], w=km)
                pq, _, kq = PS.get(512)
                P.pei("matmul", A(pq[:, :], b64_f, F[2][:, cols], start=True, stop=True), r=[("F2", tq), "cst"], w=kq)
                P.acti("activation", A(out=F[3][:, cols], in_=pm[:, :], func=AF.Copy, scale=1.0 / 64), r=km, w=[("F3", tq)])
                P.dvei("tensor_tensor", A(out=F[4][:, cols], in0=YT[:, cols], in1=F[3][:, cols], op=ALU.subtract),
                      r=YTk + [("F3", tq)], w=[("F4", tq)])
                P.acti("activation", A(out=F[5][:, cols], in_=F[3][:, cols], func=AF.Square), r=[("F3", tq)], w=[("F5", tq)])
                P.dvei("scalar_tensor_tensor", A(out=F[5][:, cols], in0=pq[:, :], scalar=1.0 / 64, in1=F[5][:, cols],
                                                                         op0=ALU.mult, op1=ALU.subtract), r=kq + [("F5", tq)], w=[("F5", tq)])
                P.dvei("tensor_single_scalar", A(out=F[5][:, cols], in_=F[5][:, cols], scalar=GN_EPS, op=ALU.add),
                      r=[("F5", tq)], w=[("F5", tq)])
                P.acti("activation", A(out=F[5][:, cols], in_=F[5][:, cols], func=AF.Sqrt), r=[("F5", tq)], w=[("F5", tq)])
                P.dvei("reciprocal", A(out=F[5][:, cols], in_=F[5][:, cols]), r=[("F5", tq)], w=[("F5", tq)])
                P.dvei("tensor_tensor", A(out=F[4][:, cols], in0=F[4][:, cols], in1=F[5][:, cols], op=ALU.mult),
                      r=[("F4", tq), ("F5", tq)], w=[("F4", tq)])
                P.dvei("tensor_scalar", A(out=F[4][:, cols], in0=F[4][:, cols], scalar1=pp[:, PP_LNG + hp:PP_LNG + hp + 1],
                                                           scalar2=pp[:, PP_LNB + hp:PP_LNB + hp + 1], op0=ALU.mult, op1=ALU.add),
                      r=[("F4", tq), "pp"], w=[("F4", tq)])
                pb_, _, kb_ = PS.get(512)
                P.pei("matmul", A(pb_[:, :], b64_f, F[1][:, cols], start=True, stop=True), r=[*FK(1), "cst"], w=kb_)
                P.dvei("tensor_tensor", A(out=F[3][:, cols], in0=pb_[:, :], in1=raw["V"][:, 1 + tq * 512:1 + (tq + 1) * 512],
                                                                    op=ALU.mult), r=kb_ + RK("V") + [("F4", tq)], w=[("F3", tq)])
                P.pooli("tensor_tensor", A(out=F[4][:, cols], in0=F[4][:, cols], in1=F[3][:, cols], op=ALU.add),
                       r=[("F4", tq), ("F3", tq)], w=[("F4", tq)])
                P.pooli("tensor_tensor", A(out=go[:, t0 + tq * 512:t0 + (tq + 1) * 512], in0=F[4][:, cols], in1=sgb[:, cols],
                                                                   op=ALU.mult), r=[("F4", tq), ("sgb", tq)], w=[("go", hf, tq)])
        P.dma("sp", G[s, 1024 + hp * 128:1024 + (hp + 1) * 128, :], go[:], r=[("go", hf, tq) for hf in range(NH) for tq in range(NTQ)],
              w=[("Gr", s, hp)])


def phase_tail(P, nc, NS, TT, x, pin, out, G, cst, pp, idb, vecd, wsTd, load_w, uib):
    NSUB = TT // 128
    ui_f = cst[:, CS_UI:CS_UI + 128]
    WsT = P.sbuf([128, 2048], BF16, "WsT")
    with P.scope():
        wsf = P.sbuf([128, 2048], F32, "wsf")
        P.dma("sp", wsf[:], wsTd, w=["wsf"])
        for g in range(16):
            P.dvei("tensor_tensor", A(out=WsT[:, g * 128:(g + 1) * 128], in0=wsf[:, g * 128:(g + 1) * 128], in1=ui_f, op=ALU.mult),
                   r=["wsf", "cst"], w=["WsT"])
        P.barrier()
    bc = [P.sbuf([128, D], F32, "bc") for _ in range(2)]
    y = P.sbuf([128, NSUB * D], F32, "y")
    xh = [P.sbuf([128, NSUB * D], F32, "xh") for _ in range(2)]
    aT = [P.sbuf([128, KC * TT], BF16, "aT") for _ in range(2)]
    pT_ = P.sbuf([128, 2 * TT], BF16, "pT")
    pt_in = P.sbuf([128, NSUB * PLE], F32, "ptin")
    pbf = P.sbuf([128, PLE], BF16, "pbf")
    NW = 3
    W = [P.sbuf([128, 8192], BF16, "W") for _ in range(NW)]
    Wp = [P.sbuf([128, 1024], BF16, "Wp") for _ in range(2)]
    gu = P.sbuf([128, NSUB * D], BF16, "gu")
    gv = P.sbuf([128, NSUB * D], BF16, "gv")
    sgt = P.sbuf([128, NSUB * D], BF16, "sgt")
    tmpf = [P.sbuf([128, 512], F32, "tmpf") for _ in range(2)]
    tmpg = [P.sbuf([128, 512], F32, "tmpg") for _ in range(2)]
    xnb = P.sbuf([128, D], BF16, "xnbt")
    t1f = P.sbuf([128, D], F32, "t1f")
    junk = P.sbuf([128, D], BF16, "junkt")
    ss = P.sbuf([128, 4], F32, "sst")
    st = P.sbuf([128, NSUB * 4 * 6], F32, "bnst")
    mv = P.sbuf([128, 4], F32, "mv")
    ptrs = [P.psum([128, 1024], BF16, "ptrt") for _ in range(2)]
    PS = PSlots(P, 6, "psr")
    cnt = {"w": 0, "wp": 0, "tr": 0, "tf": 0, "tg": 0}

    cur = {"bi": 0}

    def XK(sub):
        return ("xh", cur["bi"], sub)

    def ykeys(sub):
        return [("y", sub, c) for c in range(4)]

    def transposes(src_ap, srckeys, dst, dkey, sub, nk=KC, width=None):
        for half in range((nk + 7) // 8):
            n = min(8, nk - half * 8)
            b = cnt["tr"] % 2
            cnt["tr"] += 1
            pt_ = ptrs[b]
            for j in range(n):
                kc = half * 8 + j
                P.pei("transpose", A(pt_[:, j * 128:(j + 1) * 128], src_ap[:, kc * 128:(kc + 1) * 128], idb[:]),
                      r=srckeys + ["idb"], w=[("ptrt", b)])
            dv = dst[:, half * 8 * TT:(half * 8 + n) * TT].rearrange("p (k t) -> p k t", k=n)[:, :, sub * 128:(sub + 1) * 128]
            sv = pt_[:, 0:n * 128].rearrange("p (k t) -> p k t", k=n)
            if b == 0:
                P.acti("activation", A(out=dv, in_=sv, func=AF.Copy), r=[("ptrt", b)], w=[(dkey, sub, half)])
            else:
                P.dvei("tensor_copy", A(out=dv, in_=sv), r=[("ptrt", b)], w=[(dkey, sub, half)])

    def akeys(dkey, nk=KC):
        return [(dkey, sub, half) for sub in range(NSUB) for half in range((nk + 7) // 8)]

    def proj_tok(src, skeys, woff, nct, sink, extra=None):
        for ct in range(nct):
            b = cnt["w"] % NW
            cnt["w"] += 1
            load_w("sp", W[b][:], woff + ct * 8192, 8192, ("W", b))
            if extra is not None:
                extra("load", ct, None, None, None)
            for sub in range(NSUB):
                ps, _, keys = PS.get(512)
                for kc in range(KC):
                    P.pei("matmul", A(ps[:, :], src[:, kc * TT + sub * 128: kc * TT + (sub + 1) * 128], W[b][:, kc * 512:(kc + 1) * 512],
                                      start=(kc == 0), stop=(kc == KC - 1)), r=skeys + [("W", b)], w=keys)
                sink(ct, sub, ps, keys)

    def rstd_from_ss(col):
        c = ss[:, col:col + 1]
        P.dvei("tensor_scalar", A(out=c, in0=c, scalar1=1.0 / D, scalar2=RMS_EPS, op0=ALU.mult, op1=ALU.add), r=[("ss", col)], w=[("ss", col)])
        P.acti("activation", A(out=c, in_=c, func=AF.Sqrt), r=[("ss", col)], w=[("ss", col)])
        P.dvei("reciprocal", A(out=c, in_=c), r=[("ss", col)], w=[("ss", col)])

    def load_bc(i, voff):
        P.dma("sp", bc[i][:], vecd[voff:voff + D].partition_broadcast(128), w=[("bc", i)])

    def out_proj_norm_res(src, skeys, woff, voff, xb):
        def sink(ct, sub, ps, keys):
            P.dvei("tensor_copy", A(out=y[:, sub * D + ct * 512: sub * D + (ct + 1) * 512], in_=ps[:, :]), r=keys, w=[("y", sub, ct)])
        proj_tok(src, skeys, woff, 4, sink)
        load_bc(0, voff)
        for sub in range(NSUB):
            ys = y[:, sub * D:(sub + 1) * D]
            P.acti("activation", A(out=junk[:], in_=ys, func=AF.Square, accum_out=ss[:, sub:sub + 1]), r=ykeys(sub), w=["junk", ("ss", sub)])
            rstd_from_ss(sub)
            P.dvei("scalar_tensor_tensor", A(out=ys, in0=ys, scalar=ss[:, sub:sub + 1], in1=bc[0][:], op0=ALU.mult, op1=ALU.mult),
                   r=ykeys(sub) + [("ss", sub), ("bc", 0)], w=ykeys(sub))
            xs_ = xb[:, sub * D:(sub + 1) * D]
            P.pooli("tensor_tensor", A(out=xs_, in0=xs_, in1=ys, op=ALU.add), r=ykeys(sub) + [XK(sub)], w=[XK(sub)])

    def ple(layer, s, t0, xb, hT, hkey, woff_g, woff_p):
        for sub in range(NSUB):
            xs_ = xb[:, sub * D:(sub + 1) * D]
            P.acti("activation", A(out=xnb[:], in_=xs_, func=AF.Copy), r=[XK(sub)], w=["xnb"])
            transposes(xnb, ["xnb"], hT, hkey, sub)
        P.dma("sp", pt_in[:, :].rearrange("p (s d) -> p s d", s=NSUB),
              pin[layer, s, t0:t0 + TT, :].rearrange("(s p) d -> p s d", p=128), w=["ptin"])
        for sub in range(NSUB):
            P.dvei("tensor_copy", A(out=pbf[:], in_=pt_in[:, sub * PLE:(sub + 1) * PLE]), r=["ptin"], w=["pbf"])
            transposes(pbf, ["pbf"], pT_, "pTs", sub, nk=2)
        hk = akeys(hkey)
        pk = akeys("pTs", 2)

        def extra(kind, ct, *_):
            b = cnt["wp"] % 2
            cnt["wp"] += 1
            load_w("sp", Wp[b][:], woff_p + ct * 1024, 1024, ("Wp", b))
            extra.b = b

        def sink(ct, sub, ps, keys):
            b = extra.b
            pp_, _, kp = PS.get(512)
            for k2 in range(2):
                P.pei("matmul", A(pp_[:, :], pT_[:, k2 * TT + sub * 128: k2 * TT + (sub + 1) * 128], Wp[b][:, k2 * 512:(k2 + 1) * 512],
                                  start=(k2 == 0), stop=(k2 == 1)), r=pk + [("Wp", b)], w=kp)
            tb = cnt["tf"] % 2
            cnt["tf"] += 1
            P.acti("activation", A(out=tmpf[tb][:], in_=ps[:, :], func=AF.Sigmoid), r=keys, w=[("tmpf", tb)])
            P.dvei("tensor_tensor", A(out=tmpg[tb][:], in0=pp_[:, :], in1=tmpf[tb][:], op=ALU.mult), r=kp + [("tmpf", tb)], w=[("tmpg", tb)])
            xs_ = xb[:, sub * D + ct * 512: sub * D + (ct + 1) * 512]
            P.pooli("tensor_tensor", A(out=xs_, in0=xs_, in1=tmpg[tb][:], op=ALU.add), r=[("tmpg", tb), XK(sub)], w=[XK(sub)])
        proj_tok(hT, hk, woff_g, 4, sink, extra=extra)

    tile_i = 0
    for s in range(NS):
        for tb_ in range(T // TT):
            t0 = tb_ * TT
            xb = xh[tile_i % 2]
            cur["bi"] = tile_i % 2
            tile_i += 1
            P.dma("sp", aT[0][:, :].rearrange("p (k t) -> p k t", k=KC),
                  G[s].rearrange("(k p) t -> p k t", p=128)[:, :, t0:t0 + TT], w=akeys("gT"))
            P.dma("sp", xb[:, :].rearrange("p (s d) -> p s d", s=NSUB),
                  x[s, t0:t0 + TT, :].rearrange("(s p) d -> p s d", p=128), w=[XK(sub) for sub in range(NSUB)])
            out_proj_norm_res(aT[0], akeys("gT"), OFF_WO0, V_GPOST0, xb)
            ple(0, s, t0, xb, aT[1], "hT", OFF_WG0, OFF_WP0)
            load_bc(1, V_GPRE1)
            for sub in range(NSUB):
                xs_ = xb[:, sub * D:(sub + 1) * D]
                P.acti("activation", A(out=junk[:], in_=xs_, func=AF.Square, accum_out=ss[:, 2 + sub:3 + sub]), r=[XK(sub)], w=["junk", ("ss", 2 + sub)])
                rstd_from_ss(2 + sub)
                P.dvei("scalar_tensor_tensor", A(out=xnb[:], in0=xs_, scalar=ss[:, 2 + sub:3 + sub], in1=bc[1][:], op0=ALU.mult, op1=ALU.mult),
                       r=[XK(sub), ("ss", 2 + sub), ("bc", 1)], w=["xnb"])
                transposes(xnb, ["xnb"], aT[0], "gT", sub)

            def sink1(ct, sub, ps, keys):
                cols = slice(sub * D + (ct % 4) * 512, sub * D + (ct % 4 + 1) * 512)
                if ct < 4:
                    P.acti("activation", A(out=gu[:, cols], in_=ps[:, :], func=AF.Gelu), r=keys, w=[("gu", sub, ct % 4)])
                elif ct < 8:
                    tb = cnt["tg"] % 2
                    cnt["tg"] += 1
                    P.acti("activation", A(out=tmpf[tb][:], in_=ps[:, :], func=AF.Gelu), r=keys, w=[("tmpf", tb)])
                    si = (sub * 4 + ct % 4) * 6
                    P.dvei("bn_stats", A(out=st[:, si:si + 6], in_=tmpf[tb][:]), r=[("tmpf", tb)], w=[("st", sub, ct % 4)])
                    P.pooli("tensor_copy", A(out=gv[:, cols], in_=tmpf[tb][:]), r=[("tmpf", tb)], w=[("gv", sub, ct % 4)])
                else:
                    P.acti("activation", A(out=sgt[:, cols], in_=ps[:, :], func=AF.Silu), r=keys, w=[("sgt", sub, ct % 4)])
            proj_tok(aT[0], akeys("gT"), OFF_CIN, 12, sink1)
            load_bc(0, V_LNG)
            load_bc(1, V_LNB)
            for sub in range(NSUB):
                gk = [("gv", sub, c) for c in range(4)]
                P.dvei("bn_aggr", A(out=mv[:, 0:2], in_=st[:, sub * 24:(sub + 1) * 24].rearrange("p (c k) -> p c k", k=6)),
                       r=[("st", sub, c) for c in range(4)], w=["mv"])
                P.dvei("tensor_single_scalar", A(out=mv[:, 2:3], in_=mv[:, 1:2], scalar=LN_EPS, op=ALU.add), r=["mv"], w=["mv"])
                P.acti("activation", A(out=mv[:, 2:3], in_=mv[:, 2:3], func=AF.Sqrt), r=["mv"], w=["mv"])
                P.dvei("reciprocal", A(out=mv[:, 2:3], in_=mv[:, 2:3]), r=["mv"], w=["mv"])
                gvs = gv[:, sub * D:(sub + 1) * D]
                P.dvei("tensor_scalar", A(out=t1f[:], in0=gvs, scalar1=mv[:, 0:1], scalar2=mv[:, 2:3], op0=ALU.subtract, op1=ALU.mult),
                       r=gk + ["mv"], w=["t1f"])
                P.pooli("tensor_tensor", A(out=t1f[:], in0=t1f[:], in1=bc[0][:], op=ALU.mult), r=["t1f", ("bc", 0)], w=["t1f"])
                P.pooli("tensor_tensor", A(out=gvs, in0=t1f[:], in1=bc[1][:], op=ALU.add), r=["t1f", ("bc", 1)], w=gk)
                for g4 in range(4):
                    ps, _, keys = PS.get(512)
                    for gg in range(4):
                        g = g4 * 4 + gg
                        P.pei("matmul", A(ps[:, gg * 128:(gg + 1) * 128], WsT[:, g * 128:(g + 1) * 128], gv[:, sub * D + g * 128: sub * D + (g + 1) * 128],
                                          start=True, stop=True), r=gk + ["WsT"], w=keys)
                    tb = cnt["tg"] % 2
                    cnt["tg"] += 1
                    for gg in range(4):
                        g = g4 * 4 + gg
                        P.acti("activation", A(out=tmpf[tb][:, gg * 128:(gg + 1) * 128], in_=ps[:, gg * 128:(gg + 1) * 128], func=AF.Identity,
                                               bias=pp[:, PP_BS + g:PP_BS + g + 1]), r=keys + ["pp"], w=[("tmpf", tb)])
                    cols = slice(sub * D + g4 * 512, sub * D + (g4 + 1) * 512)
                    P.pooli("tensor_tensor", A(out=gu[:, cols], in0=tmpf[tb][:], in1=gu[:, cols], op=ALU.mult),
                            r=[("tmpf", tb), ("gu", sub, g4)], w=[("gu", sub, g4)])
                    P.pooli("tensor_tensor", A(out=gu[:, cols], in0=gu[:, cols], in1=sgt[:, cols], op=ALU.mult),
                            r=[("gu", sub, g4), ("sgt", sub, g4)], w=[("gu", sub, g4)])
                transposes(gu[:, sub * D:(sub + 1) * D], [("gu", sub, c) for c in range(4)], aT[1], "hT", sub)
            out_proj_norm_res(aT[1], akeys("hT"), OFF_CO, V_GPOST1, xb)
            ple(1, s, t0, xb, aT[0], "gT", OFF_WG1, OFF_WP1)
            P.dma("pool", out[s, t0:t0 + TT, :].rearrange("(s p) d -> p s d", p=128), xb[:, :].rearrange("p (s d) -> p s d", s=NSUB),
                  r=[XK(sub) for sub in range(NSUB)], w=[("out", s, tb_)])


_CACHE = {}


def kernel(**inputs):
    inp = {k: np.asarray(v) for k, v in inputs.items()}
    hp = host_prep(inp)
    NCORES = 8
    NS = 2
    if "nc" not in _CACHE:
        _CACHE["nc"] = build(NS=NS)
    nc = _CACHE["nc"]
    in_maps = []
    for c in range(NCORES):
        in_maps.append(dict(x=np.ascontiguousarray(inp["x"][c * NS:(c + 1) * NS]),
                            p=np.ascontiguousarray(inp["p"][:, c * NS:(c + 1) * NS]),
                            wall=hp["wall"], pp=hp["pp"], cst=hp["cst"], wsT=hp["wsT"], vec=hp["vec"]))
    res = run_bass_kernel_spmd(nc, in_maps, core_ids=list(range(NCORES)))
    return np.concatenate([r["out"] for r in res.results], axis=0).astype(np.float32)
```
